# Optimizing a Trainium2 kernel written in Bass

```python
import math
import jax, jax.numpy as jnp
from jax import lax
import numpy as np

D_MODEL = 2048
BATCH = 2
SEQ = 4096
DEPTH = 1

A_HEADS = 16
A_KV_HEADS = 4
A_HEAD_DIM = 64
A_GROUP = A_HEADS // A_KV_HEADS
WINDOW = 128
A_BLOCK = 128
A_Q_WIDTH = A_HEADS * A_HEAD_DIM
A_KV_WIDTH = A_KV_HEADS * A_HEAD_DIM
N_BUCKETS = 32
MAX_DISTANCE = 128
B_HEADS = 4
B_KEY_DIM = 128
B_VAL_DIM = 256
B_QK_WIDTH = B_HEADS * B_KEY_DIM
B_V_WIDTH = B_HEADS * B_VAL_DIM
B_GATE_RANK = 16
B_GATE_TAU = 16.0
B_CHUNK = 64
D_FF = -(-(8 * D_MODEL) // (3 * 256)) * 256
EPS = 1e-6
NEG_INF = -1e30

IN_SPLITS = (A_Q_WIDTH, A_KV_WIDTH, A_KV_WIDTH,
             B_QK_WIDTH, B_QK_WIDTH, B_V_WIDTH, B_GATE_RANK, B_V_WIDTH,
             D_MODEL, D_MODEL)
IN_WIDTH = sum(IN_SPLITS)

kernel_name = "hybrid_swa_sink_gla_gated_merge"


def rmsnorm(x, g):
    xf = x.astype(jnp.float32)
    y = xf * lax.rsqrt(jnp.mean(xf * xf, axis=-1, keepdims=True) + EPS)
    return (y * g.astype(jnp.float32)).astype(x.dtype)


def t5_causal_bucket(dist):
    max_exact = N_BUCKETS // 2
    d = jnp.maximum(dist, 0)
    large = max_exact + (jnp.log(jnp.maximum(d, 1).astype(jnp.float32) / max_exact)
                         / math.log(MAX_DISTANCE / max_exact)
                         * (N_BUCKETS - max_exact)).astype(jnp.int32)
    large = jnp.minimum(large, N_BUCKETS - 1)
    return jnp.where(d < max_exact, d, large)


def sliding_window_attention(q, k, v, sinks, rel_bias):
    B, S = q.shape[0], q.shape[1]
    nb = S // A_BLOCK
    qb = (q * (A_HEAD_DIM ** -0.5)).reshape(B, nb, A_BLOCK, A_KV_HEADS, A_GROUP, A_HEAD_DIM)
    pad = ((0, 0), (A_BLOCK, 0), (0, 0), (0, 0))
    kp = jnp.pad(k, pad).reshape(B, nb + 1, A_BLOCK, A_KV_HEADS, A_HEAD_DIM)
    vp = jnp.pad(v, pad).reshape(B, nb + 1, A_BLOCK, A_KV_HEADS, A_HEAD_DIM)
    kb = jnp.concatenate([kp[:, :-1], kp[:, 1:]], axis=2)
    vb = jnp.concatenate([vp[:, :-1], vp[:, 1:]], axis=2)
    s = jnp.einsum('bnqkgd,bnskd->bnkgqs', qb, kb).astype(jnp.float32)
    q_loc = jnp.arange(A_BLOCK)[:, None] + A_BLOCK
    k_loc = jnp.arange(2 * A_BLOCK)[None, :]
    dist = q_loc - k_loc
    band = (dist >= 0) & (dist < WINDOW)
    k_abs = jnp.arange(nb)[:, None, None] * A_BLOCK + k_loc[None] - A_BLOCK
    mask = band[None] & (k_abs >= 0)
    bias = rel_bias.astype(jnp.float32)[t5_causal_bucket(dist)]
    bias = bias.transpose(2, 0, 1).reshape(A_KV_HEADS, A_GROUP, A_BLOCK, 2 * A_BLOCK)
    s = jnp.where(mask[None, :, None, None], s + bias[None, None], NEG_INF)
    sink = sinks.astype(jnp.float32).reshape(A_KV_HEADS, A_GROUP)[None, None, :, :, None, None]
    m = jnp.maximum(jnp.max(s, axis=-1, keepdims=True), sink)
    p = jnp.exp(s - m)
    denom = jnp.sum(p, axis=-1, keepdims=True) + jnp.exp(sink - m)
    o = jnp.einsum('bnkgqs,bnskd->bnqkgd', (p / denom).astype(v.dtype), vb)
    return o.reshape(B, S, A_Q_WIDTH)


def gated_linear_attention(q, k, v, log_a):
    B, S = q.shape[0], q.shape[1]
    nc = S // B_CHUNK

    def chunks(t):
        return t.astype(jnp.float32).reshape(B, nc, B_CHUNK, B_HEADS, -1).transpose(0, 3, 1, 2, 4)

    qc = chunks(q) * (B_KEY_DIM ** -0.5)
    kc, vc, gc = chunks(k), chunks(v), chunks(log_a)
    b = jnp.cumsum(gc, axis=3)
    b_last = b[:, :, :, -1:, :]
    q_dec = qc * jnp.exp(b)
    k_dec = kc * jnp.exp(-b)
    k_state = kc * jnp.exp(b_last - b)
    causal = jnp.tril(jnp.ones((B_CHUNK, B_CHUNK), dtype=bool))
    att = jnp.where(causal, jnp.einsum('bhncd,bhnsd->bhncs', q_dec, k_dec), 0.0)
    o_intra = jnp.einsum('bhncs,bhnsv->bhncv', att, vc)
    dS = jnp.einsum('bhncd,bhncv->bhndv', k_state, vc)
    decay = jnp.exp(b_last[:, :, :, 0, :])

    def step(state, inp):
        dS_n, decay_n = inp
        return decay_n[..., None] * state + dS_n, state

    s0 = jnp.zeros((B, B_HEADS, B_KEY_DIM, B_VAL_DIM), jnp.float32)
    _, s_prev = lax.scan(step, s0, (dS.transpose(2, 0, 1, 3, 4), decay.transpose(2, 0, 1, 3)))
    s_prev = s_prev.transpose(1, 2, 0, 3, 4)
    o_inter = jnp.einsum('bhncd,bhndv->bhncv', q_dec, s_prev)
    o = o_intra + o_inter
    return o.transpose(0, 2, 3, 1, 4).reshape(B, S, B_HEADS, B_VAL_DIM)


def setup_inputs(seed: int = 0) -> dict:
    key = jax.random.key(seed)
    ks = jax.random.split(key, 20)
    L = DEPTH

    def w(k, shape, fan_in):
        return jax.random.normal(k, shape, jnp.float32) * (fan_in ** -0.5)

    def gain(k, shape):
        return 1.0 + 0.02 * jax.random.normal(k, shape, jnp.float32)

    return {
        "x": jax.random.normal(ks[0], (BATCH, SEQ, D_MODEL), jnp.float32),
        "norm_mix_g": gain(ks[1], (L, D_MODEL)),
        "w_in": w(ks[2], (L, D_MODEL, IN_WIDTH), D_MODEL),
        "sinks": 0.5 * jax.random.normal(ks[3], (L, A_HEADS), jnp.float32),
        "rel_bias": 0.5 * jax.random.normal(ks[4], (N_BUCKETS, A_HEADS), jnp.float32),
        "w_gate_up": w(ks[5], (L, B_GATE_RANK, B_QK_WIDTH), B_GATE_RANK),
        "b_gate": 0.1 * jax.random.normal(ks[6], (L, B_QK_WIDTH), jnp.float32),
        "gla_norm_g": gain(ks[7], (L, B_VAL_DIM)),
        "w_proj_a": w(ks[8], (L, A_Q_WIDTH, D_MODEL), A_Q_WIDTH),
        "w_proj_b": w(ks[9], (L, B_V_WIDTH, D_MODEL), B_V_WIDTH),
        "w_out": w(ks[10], (L, D_MODEL, D_MODEL), D_MODEL),
        "norm_ffn_g": gain(ks[11], (L, D_MODEL)),
        "w_ffn_gate": w(ks[12], (L, D_MODEL, D_FF), D_MODEL),
        "w_ffn_up": w(ks[13], (L, D_MODEL, D_FF), D_MODEL),
        "w_ffn_down": w(ks[14], (L, D_FF, D_MODEL), D_FF),
        "norm_final_g": gain(ks[15], (D_MODEL,)),
    }


def reference(x, norm_mix_g, w_in, sinks, rel_bias, w_gate_up, b_gate, gla_norm_g,
              w_proj_a, w_proj_b, w_out, norm_ffn_g, w_ffn_gate, w_ffn_up, w_ffn_down,
              norm_final_g):
    B, S = x.shape[0], x.shape[1]
    split_idx = [int(i) for i in np.cumsum(IN_SPLITS)[:-1]]
    h = x
    for l in range(DEPTH):
        u = rmsnorm(h, norm_mix_g[l])
        proj = u @ w_in[l]
        qa, ka, va, qb, kb, vb, g_low, ob_gate, gate_a, gate_b = jnp.split(proj, split_idx, axis=-1)
        ya = sliding_window_attention(
            qa.reshape(B, S, A_HEADS, A_HEAD_DIM),
            ka.reshape(B, S, A_KV_HEADS, A_HEAD_DIM),
            va.reshape(B, S, A_KV_HEADS, A_HEAD_DIM),
            sinks[l], rel_bias)
        ya = ya @ w_proj_a[l]
        log_a = jax.nn.log_sigmoid((g_low @ w_gate_up[l] + b_gate[l]).astype(jnp.float32)) / B_GATE_TAU
        ob = gated_linear_attention(
            qb.reshape(B, S, B_HEADS, B_KEY_DIM),
            kb.reshape(B, S, B_HEADS, B_KEY_DIM),
            vb.reshape(B, S, B_HEADS, B_VAL_DIM),
            log_a.reshape(B, S, B_HEADS, B_KEY_DIM))
        ob = rmsnorm(ob, gla_norm_g[l]).reshape(B, S, B_V_WIDTH).astype(x.dtype)
        yb = (ob * jax.nn.silu(ob_gate)) @ w_proj_b[l]
        merged = jax.nn.sigmoid(gate_a) * ya + jax.nn.sigmoid(gate_b) * yb
        h = h + (merged @ w_out[l]).astype(h.dtype)
        z = rmsnorm(h, norm_ffn_g[l])
        ff = (jax.nn.silu(z @ w_ffn_gate[l]) * (z @ w_ffn_up[l])) @ w_ffn_down[l]
        h = h + ff.astype(h.dtype)
    return rmsnorm(h, norm_final_g)
```

```python
import contextlib
import types
import numpy as np
import concourse.bass as bass
import concourse.mybir as mybir
from concourse.bass_utils import run_bass_kernel_spmd

F32 = mybir.dt.float32
BF = mybir.dt.bfloat16
AF = mybir.ActivationFunctionType
ALU = mybir.AluOpType

D = 2048
NT = 8
TOK = 1024
NPREV = 3072
IN_W = 8720
DFF = 5632
NEG = -30000.0
C_QA, C_KA, C_VA, C_QB, C_KB, C_VB, C_G, C_OG, C_GA, C_GB = 0, 1024, 1280, 1536, 2048, 2560, 3584, 3600, 4624, 6672

ENGS = ("pe", "act", "dve", "pool", "sp")


class Res:
    __slots__ = ("name", "last_w", "readers")

    def __init__(self, name, after=()):
        self.name = name
        self.last_w = None
        self.readers = []
        for o in after:
            if o.last_w is not None:
                self.readers.append(o.last_w)
            self.readers.extend(o.readers)


class Op:
    __slots__ = ("eng", "fn", "deps", "idx", "signal", "sem_key", "count", "is_dma", "name")


def _freeze(fn):
    if getattr(fn, "__closure__", None) is None:
        return fn
    cells = []
    for c in fn.__closure__:
        try:
            cells.append(types.CellType(c.cell_contents))
        except ValueError:
            cells.append(c)
    g = types.FunctionType(fn.__code__, fn.__globals__, fn.__name__, fn.__defaults__, tuple(cells))
    g.__kwdefaults__ = fn.__kwdefaults__
    return g


class Sched:
    def __init__(self, nc):
        self.nc = nc
        self.ops = []
        self.eng_ops = {e: [] for e in ENGS}
        self.sem_total = {}
        self.n_waits = 0

    def _add(self, eng, fn, reads, writes, is_dma, key, name):
        op = Op()
        op.eng, op.fn, op.is_dma, op.name = eng, _freeze(fn), is_dma, name
        op.idx = len(self.ops)
        op.signal = False
        op.count = None
        deps = []
        for r in reads:
            if r.last_w is not None:
                deps.append(r.last_w)
        for w in writes:
            if w.last_w is not None:
                deps.append(w.last_w)
            deps.extend(w.readers)
        out = []
        seen = set()
        for d in deps:
            if d is op or id(d) in seen:
                continue
            seen.add(id(d))
            if (not d.is_dma) and (not is_dma) and d.eng == eng and eng == "pe":
                continue
            out.append(d)
        op.deps = out
        for d in out:
            d.signal = True
        if is_dma:
            op.sem_key = ("dma", key if key is not None else writes[0].name)
            op.signal = True
        else:
            op.sem_key = ("eng", eng)
        for r in reads:
            r.readers.append(op)
        for w in writes:
            w.last_w = op
            w.readers = []
        self.ops.append(op)
        self.eng_ops[eng].append(op)
        return op

    def op(self, eng, fn, reads=(), writes=(), name=None):
        return self._add(eng, fn, list(reads), list(writes), False, None, name)

    def dma(self, fn, reads=(), writes=(), queue="sp", key=None, name=None):
        return self._add(queue, fn, list(reads), list(writes), True, key, name)

    def emit(self, final_wait_ops=()):
        nc = self.nc
        for op in self.ops:
            if op.signal:
                inc = 16 if op.is_dma else 1
                t = self.sem_total.get(op.sem_key, 0) + inc
                self.sem_total[op.sem_key] = t
                op.count = t
        keys = list(self.sem_total.keys())
        with contextlib.ExitStack() as st:
            sems = {}
            for i, k in enumerate(keys):
                sems[k] = st.enter_context(nc.semaphore("s%d" % i))
            block = st.enter_context(nc.Block())
            engobj = {"pe": block.tensor, "act": block.scalar, "dve": block.vector,
                      "pool": block.gpsimd, "sp": block.sync}
            sched = self

            def make(ename):
                def body(e):
                    known = {}
                    for op in sched.eng_ops[ename]:
                        need = {}
                        for d in op.deps:
                            k = d.sem_key
                            if d.count > need.get(k, 0):
                                need[k] = d.count
                        for k, v in need.items():
                            if known.get(k, 0) >= v:
                                continue
                            e.wait_ge(sems[k], v)
                            sched.n_waits += 1
                            known[k] = v
                        inst = op.fn(e)
                        if op.signal:
                            assert inst is not None, op.name
                            inst.then_inc(sems[op.sem_key], 16 if op.is_dma else 1)
                    if ename == "sp":
                        for fop in final_wait_ops:
                            e.wait_ge(sems[fop.sem_key], fop.count)
                return body

            for ename in ENGS:
                engobj[ename](make(ename))


def build(dbg=None, stop=None):
    dbg = dbg or {}
    nc = bass.Bass("TRN2", target_bir_lowering=False)

    def din(name, shape):
        return nc.dram_tensor(name, list(shape), F32, kind="ExternalInput")

    x_own = din("x_own", [TOK, D])
    x_prev = din("x_prev", [NPREV, D])
    flag_d = din("flag", [128, 1])
    w_in = din("w_in", [D, IN_W])
    w_pa = din("w_proj_a", [1024, D])
    w_pb = din("w_proj_b", [1024, D])
    w_out = din("w_out", [D, D])
    w_fg = din("w_ffn_gate", [D, DFF])
    w_fu = din("w_ffn_up", [D, DFF])
    w_fd = din("w_ffn_down", [DFF, D])
    gmix_d = din("gmix_cols", [128, 16])
    gffn_d = din("gffn_cols", [128, 16])
    bgate_d = din("bgate_cols", [128, 4])
    gfin_d = din("gfinal", [1, D])
    gnorm_d = din("glanorm", [1, 256])
    sink_d = din("sinks_l", [128, 8])
    mbias_d = din("mbias", [128, 4, 2, 512])
    wgu_d = din("w_gate_up", [16, 512])
    out_d = nc.dram_tensor("out", [TOK, D], F32, kind="ExternalOutput")
    dbg_t = {}
    for k, shp in dbg.items():
        if k.startswith("_"):
            continue
        dbg_t[k] = nc.dram_tensor("dbg_" + k, list(shp), F32, kind="ExternalOutput")

    S = Sched(nc)
    st = contextlib.ExitStack()
    with st:
        def sb(name, shape, dt=F32):
            return st.enter_context(nc.sbuf_tensor(name, list(shape), dt))

        CONST_W = 8192
        constR = sb("constR", [128, CONST_W])
        wbfR = sb("wbfR", [128, 8192])
        wstR = sb("wstR", [128, 4096])
        xR = sb("xR", [128, 6144])
        utR = sb("utR", [128, 9216])
        arR = sb("arR", [128, 16000])
        banks = [st.enter_context(nc.psum_tensor("bank%d" % i, [128, 512], F32)) for i in range(8)]
        bankres = [Res("bank%d" % i) for i in range(8)]
        bank_rr = [0]

        def next_bank():
            i = bank_rr[0] % 8
            bank_rr[0] += 1
            return banks[i], bankres[i]

        def vw(region, off_words, n_words, dt=F32, pat=None, parts=None, **kw):
            ap = region[:, off_words:off_words + n_words] if parts is None else \
                region[parts[0]:parts[1], off_words:off_words + n_words]
            if dt != F32:
                ap = ap.bitcast(dt)
            if pat is not None:
                ap = ap.rearrange(pat, **kw)
            return ap

        cofs = [0]

        def calloc(n):
            o = cofs[0]
            cofs[0] += n
            assert cofs[0] <= CONST_W
            return o

        o_ident = calloc(64)
        o_identf = calloc(128)
        o_maskT = calloc(128)
        o_maskS = calloc(1024)
        o_ones = calloc(32)
        o_gmix = calloc(16)
        o_gffn = calloc(16)
        o_bg = calloc(4)
        o_nbg = calloc(4)
        o_gnb = calloc(256)
        o_es = calloc(8)
        o_esb = calloc(1024)
        o_flag = calloc(1)
        o_wgu = calloc(256)
        o_mb = calloc(4096)
        o_S = calloc(1024)
        o_small = calloc(64)
        ident = vw(constR, o_ident, 64, BF)
        identf = vw(constR, o_identf, 128)
        maskT = vw(constR, o_maskT, 128)
        maskS = vw(constR, o_maskS, 1024)
        ones_bf = vw(constR, o_ones, 32, BF)
        gmix = vw(constR, o_gmix, 16)
        gffn = vw(constR, o_gffn, 16)
        bg = vw(constR, o_bg, 4)
        nbg = vw(constR, o_nbg, 4)
        gnb = vw(constR, o_gnb, 256)
        es = vw(constR, o_es, 8)
        esb = vw(constR, o_esb, 1024, pat="p (s q) -> p s q", q=128)
        flag = vw(constR, o_flag, 1)
        wguf = vw(utR, 6144, 512, parts=(0, 16))
        wgu = vw(constR, o_wgu, 256, BF, parts=(0, 16))
        mb = vw(constR, o_mb, 4096, pat="p (g c n) -> p g c n", g=4, c=2)
        Sst = vw(constR, o_S, 1024, pat="p (h v) -> p h v", h=4)
        small = vw(constR, o_small, 64)
        R_const = Res("const")
        R_mb = Res("mb")
        R_S = [Res("S%d" % h) for h in range(4)]
        R_smc = [Res("small%d" % i) for i in range(64)]
        small_rr = [0]

        def small_col():
            i = small_rr[0] % 64
            small_rr[0] += 1
            return i
        o_eps = calloc(1)
        eps_col = vw(constR, o_eps, 1)

        wbf_res = [Res("wbf0"), Res("wbf1")]
        wst_res = [Res("wst0"), Res("wst1")]
        wbf_rr = [0]
        wst_rr = [0]

        def out_dbg(name, src_ap, res_list, dram_ap=None):
            if name in dbg_t:
                d = dbg_t[name].ap() if dram_ap is None else dram_ap
                o = S.dma(lambda e, d=d, s=src_ap: e.dma_start(out=d, in_=s), reads=res_list,
                          writes=[Res("dbgo")], key="dbg_" + name + str(len(S.ops)))
                final_ops.append(o)

        final_ops = []

        def ld_const(dst, src, nm):
            S.dma(lambda e: e.dma_start(out=dst, in_=src), writes=[R_const], key="c")

        ld_const(gmix, gmix_d.ap(), "gmix")
        ld_const(gffn, gffn_d.ap(), "gffn")
        ld_const(bg, bgate_d.ap(), "bg")
        ld_const(gnb, bass.AP(gnorm_d, 0, [[0, 128], [1, 256]]), "gnb")
        ld_const(es, sink_d.ap(), "es")
        ld_const(flag, flag_d.ap(), "flag")
        ld_const(wguf, wgu_d.ap(), "wguf")
        S.op("pool", lambda e: e.memset(identf, 1.0), writes=[R_const])
        S.op("pool", lambda e: e.affine_select(out=identf, in_=identf, pattern=[[1, 128]], compare_op=ALU.is_equal,
                                               fill=0.0, base=0, channel_multiplier=-1), reads=[R_const], writes=[R_const])
        S.op("pool", lambda e: e.memset(maskT, 1.0), reads=[R_const], writes=[R_const])
        S.op("pool", lambda e: e.affine_select(out=maskT, in_=maskT, pattern=[[1, 128]], compare_op=ALU.is_ge,
                                               fill=0.0, base=0, channel_multiplier=-1), reads=[R_const], writes=[R_const])
        S.op("pool", lambda e: e.memset(maskT[0:64, 64:128], 0.0), reads=[R_const], writes=[R_const])
        S.op("pool", lambda e: e.memset(maskS, 1.0), reads=[R_const], writes=[R_const])
        S.op("pool", lambda e: e.memset(maskS.rearrange("p (a b) -> p a b", b=64)[:, :, 0:1], 0.0),
             reads=[R_const], writes=[R_const])
        S.op("pool", lambda e: e.memset(ones_bf, 1.0), reads=[R_const], writes=[R_const])
        S.op("pool", lambda e: e.memset(eps_col, 1e-6), reads=[R_const], writes=[R_const])
        S.op("pool", lambda e: e.memset(Sst, 0.0), reads=[R_const], writes=[R_const] + R_S)
        S.op("dve", lambda e: e.tensor_copy(ident, identf), reads=[R_const], writes=[R_const])
        S.op("dve", lambda e: e.tensor_copy(wgu, wguf), reads=[R_const], writes=[R_const])
        S.op("dve", lambda e: e.tensor_scalar(nbg, bg, -1.0, None, ALU.mult), reads=[R_const], writes=[R_const])
        S.op("act", lambda e: e.activation(out=es, in_=es, func=AF.Exp), reads=[R_const], writes=[R_const])
        S.op("dve", lambda e: e.tensor_copy(esb, es[:, :, None].broadcast_to([128, 8, 128])), reads=[R_const], writes=[R_const])
        for g_ in range(4):
            S.dma(lambda e, g_=g_: e.dma_start(out=mb[:, g_, :, :], in_=mbias_d.ap()[:, g_, :, :]), writes=[R_mb], key="mb")
        def load_unit(pieces, n_words_bf):
            si = wbf_rr[0] % 2
            wbf_rr[0] += 1
            res = wbf_res[si]
            base = si * 4096
            for (dap, (a, b), dst_fn, gain) in pieces:
                ti = wst_rr[0] % 2
                wst_rr[0] += 1
                sres = wst_res[ti]
                stg = vw(wstR, ti * 2048, a * b, pat="p (a b) -> p a b", a=a)
                S.dma(lambda e, stg=stg, dap=dap: e.dma_start(out=stg, in_=dap), writes=[sres], key="wst%d" % ti)
                dst = dst_fn(base)
                if gain is None:
                    S.op("pool", lambda e, dst=dst, stg=stg: e.tensor_copy(dst, stg), reads=[sres], writes=[res])
                else:
                    gt, kc0 = gain
                    for k in range(a):
                        S.op("pool", lambda e, dst=dst, stg=stg, k=k, gt=gt, kc0=kc0:
                             e.tensor_scalar(dst[:, k, :], stg[:, k, :], gt[:, kc0 + k:kc0 + k + 1], 1.0, ALU.mult, ALU.mult),
                             reads=[sres, R_const], writes=[res])
            return base, res

        def wslot(base, n_kc, ncols, off_words=0):
            return vw(wbfR, base + off_words, n_kc * ncols // 2, BF, pat="p (k c) -> p k c", k=n_kc)

        def std_unit(W, ncolsW, col0, ncols, gain_tile, n_kc=16, row0=0):
            pieces = []
            kpp = 2048 // ncols if ncols * 4 <= 2048 else 1
            kpp = min(kpp, n_kc)
            for p0 in range(0, n_kc, kpp):
                dap = bass.AP(W, (row0 + p0 * 128) * ncolsW + col0, [[ncolsW, 128], [128 * ncolsW, kpp], [1, ncols]])
                pieces.append((dap, (kpp, ncols),
                               (lambda base, p0=p0: wslot(base, n_kc, ncols)[:, p0:p0 + kpp, :]),
                               (gain_tile, p0) if gain_tile is not None else None))
            base, res = load_unit(pieces, n_kc * ncols // 2)
            return wslot(base, n_kc, ncols), res

        def mm_group(e, out_ap, pairs):
            n = len(pairs)
            inst = None
            for i, (l, r) in enumerate(pairs):
                inst = e.matmul(out_ap, lhsT=l, rhs=r, start=(i == 0), stop=(i == n - 1))
            return inst

        xst_res = [Res("xst0"), Res("xst1")]
        ubf_res = [Res("ubf0"), Res("ubf1")]
        xst_rr = [0]

        def rstd_from_ss(ss_col, n, out_col):
            S.op("act", lambda e: e.activation(out=small[:, out_col:out_col + 1], in_=small[:, ss_col:ss_col + 1],
                                               func=AF.Ln, scale=1.0 / n, bias=eps_col),
                 reads=[R_smc[ss_col], R_const], writes=[R_smc[out_col]])
            S.op("act", lambda e: e.activation(out=small[:, out_col:out_col + 1], in_=small[:, out_col:out_col + 1],
                                               func=AF.Exp, scale=-0.5), reads=[R_smc[out_col]], writes=[R_smc[out_col]])

        def norm_transpose(src_kind, src, dstT_fn, dst_res, src_res=None):
            i = xst_rr[0] % 2
            xst_rr[0] += 1
            if src_kind == "dram":
                xs = vw(xR, i * 2048, 2048)
                S.dma(lambda e: e.dma_start(out=xs, in_=src), writes=[xst_res[i]], key="xst%d" % i)
                xres = xst_res[i]
            else:
                xs = src
                xres = src_res
            ub = vw(xR, 4096 + i * 1024, 1024, BF)
            c0, c1 = small_col(), small_col()
            junk = vw(xR, 4096 + i * 1024, 1024, BF)
            S.op("act", lambda e: e.activation(out=junk, in_=xs, func=AF.Square, accum_out=small[:, c0:c0 + 1]),
                 reads=[xres], writes=[ubf_res[i], R_smc[c0]])
            rstd_from_ss(c0, 2048.0, c1)
            S.op("act", lambda e: e.activation(out=ub, in_=xs, func=AF.Copy, scale=small[:, c1:c1 + 1]),
                 reads=[xres, R_smc[c1]], writes=[ubf_res[i]])
            for hb in range(2):
                bk, bkr = next_bank()
                bkb = bk[:].bitcast(BF).rearrange("p (k c) -> p k c", k=8)

                def tr(e, bkb=bkb, hb=hb, ub=ub):
                    inst = None
                    for k in range(8):
                        kc = hb * 8 + k
                        inst = e.transpose(bkb[:, k, :], ub[:, kc * 128:(kc + 1) * 128], ident)
                    return inst
                S.op("pe", tr, reads=[ubf_res[i], R_const], writes=[bkr])
                dst = dstT_fn(hb)
                eng = "dve" if hb == 0 else "act"
                if eng == "dve":
                    S.op("dve", lambda e, dst=dst, bkb=bkb: e.tensor_copy(dst, bkb), reads=[bkr], writes=[dst_res])
                else:
                    S.op("act", lambda e, dst=dst, bkb=bkb: e.activation(out=dst, in_=bkb, func=AF.Copy),
                         reads=[bkr], writes=[dst_res])

        def gla_factors(h, glT, glres, T, A, B, dec, tmp_res, want_eb):
            for hf in range(T // 512):
                bk, bkr = next_bank()
                S.op("pe", lambda e, bk=bk, hf=hf: e.matmul(bk[:, :], lhsT=wgu[:, h * 128:(h + 1) * 128],
                                                            rhs=glT[:, hf * 512:(hf + 1) * 512], start=True, stop=True),
                     reads=[glres, R_const], writes=[bkr])
                S.op("act", lambda e, bk=bk, hf=hf: e.activation(out=A[:, hf * 512:(hf + 1) * 512], in_=bk[:, :], func=AF.Exp,
                                                                  scale=-1.0, bias=nbg[:, h:h + 1]),
                     reads=[bkr, R_const], writes=[tmp_res])
            S.op("act", lambda e: e.activation(out=A, in_=A, func=AF.Ln, bias=1.0), reads=[tmp_res], writes=[tmp_res])
            S.op("dve", lambda e: e.tensor_tensor_scan(B, maskS[:, 0:T], A, 0.0, ALU.mult, ALU.add),
                 reads=[tmp_res, R_const], writes=[tmp_res])
            S.op("act", lambda e: e.activation(out=dec, in_=B.rearrange("p (n c) -> p n c", c=64)[:, :, 63], func=AF.Exp,
                                               scale=-1.0 / 16), reads=[tmp_res], writes=[tmp_res])
            if want_eb:
                S.op("act", lambda e: e.activation(out=A, in_=B, func=AF.Exp, scale=-1.0 / 16), reads=[tmp_res], writes=[tmp_res])
            S.op("act", lambda e: e.activation(out=B, in_=B, func=AF.Exp, scale=1.0 / 16), reads=[tmp_res], writes=[tmp_res])

        if stop == 'setup':
            S.emit(final_wait_ops=final_ops)
            build.stats = (len(S.ops), S.n_waits, len(S.sem_total))
            return nc
        NPC = 1552
        wp = vw(arR, 0, 16 * NPC // 2, BF, pat="p (k c) -> p k c", k=16)
        R_wp = Res("wp")
        for kc in range(16):
            ti = wst_rr[0] % 2
            wst_rr[0] += 1
            stg = vw(wstR, ti * 2048, NPC)
            dap = bass.AP(w_in, (kc * 128) * IN_W + C_KB, [[IN_W, 128], [1, NPC]])
            S.dma(lambda e, stg=stg, dap=dap: e.dma_start(out=stg, in_=dap), writes=[wst_res[ti]], key="wst%d" % ti)
            S.op("pool", lambda e, stg=stg, kc=kc: e.tensor_scalar(wp[:, kc, :], stg, gmix[:, kc:kc + 1], 1.0, ALU.mult, ALU.mult),
                 reads=[wst_res[ti], R_const], writes=[R_wp])
        AO = 16 * NPC // 2
        p_sp = vw(arR, AO, 512)
        p_einv = vw(arR, AO + 512, 512)
        p_kd = vw(arR, AO + 1024, 512)
        p_dec = vw(arR, AO + 1536, 8)
        p_kstT = vw(arR, AO + 1600, 256, BF)
        p_kst = vw(arR, AO + 1856, 256, BF, pat="p (t d) -> p t d", t=4)
        p_glT = vw(arR, AO + 2112, 256, BF, parts=(0, 16))
        assert AO + 2368 <= 16000
        uTs = vw(utR, 0, 4096, BF, pat="p (k t) -> p k t", k=16)
        p_vtok = vw(utR, 4096, 2048, BF, pat="p (t c) -> p t c", t=4)
        R_uTs = [Res("uTs%d" % t) for t in range(4)]
        R_ptmp = Res("ptmp")
        R_pkst = Res("pkst")
        R_pgl = Res("pgl")
        R_pv = Res("pv")
        n_super = dbg.get("_n_super", NPREV // 512)
        for sti in range(NPREV // 512 - n_super, NPREV // 512):
            for t in range(4):
                r0 = sti * 512 + t * 128
                norm_transpose("dram", x_prev.ap()[r0:r0 + 128, :],
                               (lambda hb, t=t: uTs[:, hb * 8:(hb + 1) * 8, t * 128:(t + 1) * 128]), R_uTs[t])
            bk, bkr = next_bank()
            S.op("pe", lambda e, bk=bk: mm_group(e, bk[0:16, :], [(wp[:, kc, 1536:1552], uTs[:, kc, :]) for kc in range(16)]),
                 reads=R_uTs + [R_wp], writes=[bkr])
            S.op("act", lambda e, bk=bk: e.activation(out=p_glT, in_=bk[0:16, :], func=AF.Copy), reads=[bkr], writes=[R_pgl])
            for t in range(4):
                for cb in range(2):
                    bk, bkr = next_bank()
                    S.op("pe", lambda e, bk=bk, t=t, cb=cb: mm_group(
                        e, bk[:, :], [(uTs[:, kc, t * 128:(t + 1) * 128], wp[:, kc, 512 + cb * 512:1024 + cb * 512]) for kc in range(16)]),
                        reads=[R_uTs[t], R_wp], writes=[bkr])
                    eng = "dve" if cb == 0 else "act"
                    dst = p_vtok[:, t, cb * 512:(cb + 1) * 512]
                    if eng == "dve":
                        S.op("dve", lambda e, dst=dst, bk=bk: e.tensor_copy(dst, bk[:, :]), reads=[bkr], writes=[R_pv])
                    else:
                        S.op("act", lambda e, dst=dst, bk=bk: e.activation(out=dst, in_=bk[:, :], func=AF.Copy), reads=[bkr], writes=[R_pv])
            for h in range(4):
                gla_factors(h, p_glT, R_pgl, 512, p_sp, p_einv, p_dec, R_ptmp, False)
                bk, bkr = next_bank()
                S.op("pe", lambda e, bk=bk, h=h: mm_group(e, bk[:, :], [(wp[:, kc, h * 128:(h + 1) * 128], uTs[:, kc, :]) for kc in range(16)]),
                     reads=R_uTs + [R_wp], writes=[bkr])
                S.op("dve", lambda e, bk=bk: e.tensor_tensor(p_kd, bk[:, :], p_einv, ALU.mult), reads=[bkr, R_ptmp], writes=[R_ptmp])
                S.op("dve", lambda e: e.tensor_tensor(p_kstT.rearrange("p (n c) -> p n c", c=64),
                                                      p_kd.rearrange("p (n c) -> p n c", c=64),
                                                      p_dec[:, :, None].broadcast_to([128, 8, 64]), ALU.mult),
                     reads=[R_ptmp], writes=[R_ptmp])
                bk, bkr = next_bank()
                bkb = bk[:].bitcast(BF).rearrange("p (k c) -> p k c", k=8)

                def trk(e, bkb=bkb):
                    inst = None
                    for t in range(4):
                        inst = e.transpose(bkb[:, t, :], p_kstT[:, t * 128:(t + 1) * 128], ident)
                    return inst
                S.op("pe", trk, reads=[R_ptmp, R_const], writes=[bkr])
                S.op("act", lambda e, bkb=bkb: e.activation(out=p_kst, in_=bkb[:, 0:4, :], func=AF.Copy), reads=[bkr], writes=[R_pkst])
                for n in range(8):
                    t, hf = n // 2, n % 2
                    bk, bkr = next_bank()
                    S.op("pe", lambda e, bk=bk, t=t, hf=hf, h=h: e.matmul(
                        bk[:, 0:256], lhsT=p_kst[hf * 64:(hf + 1) * 64, t, :],
                        rhs=p_vtok[hf * 64:(hf + 1) * 64, t, h * 256:(h + 1) * 256], start=True, stop=True),
                        reads=[R_pkst, R_pv], writes=[bkr])
                    S.op("dve", lambda e, bk=bk, n=n, h=h: e.scalar_tensor_tensor(
                        Sst[:, h, :], Sst[:, h, :], p_dec[:, n:n + 1], bk[:, 0:256], ALU.mult, ALU.add),
                        reads=[bkr, R_ptmp, R_S[h]], writes=[R_S[h]])
        out_dbg("S_in", Sst, R_S)
        prologue_res = [R_wp, R_ptmp, R_pkst, R_pgl, R_pv] + R_uTs

        if stop == 'prologue':
            S.emit(final_wait_ops=final_ops)
            build.stats = (len(S.ops), S.n_waits, len(S.sem_total))
            return nc
        uT = vw(utR, 0, 9216, BF, pat="p (k t) -> p k t", k=16)
        R_uT = [Res("uT%d" % t, after=prologue_res + [R_const]) for t in range(9)]
        for t in range(9):
            src = x_prev.ap()[NPREV - 128:NPREV, :] if t == 0 else x_own.ap()[(t - 1) * 128:t * 128, :]
            norm_transpose("dram", src, (lambda hb, t=t: uT[:, hb * 8:(hb + 1) * 8, t * 128:(t + 1) * 128]), R_uT[t])

        def uT_res(t0, t1):
            return R_uT[t0:t1]

        if stop == 'a1':
            S.emit(final_wait_ops=final_ops)
            build.stats = (len(S.ops), S.n_waits, len(S.sem_total))
            return nc
        a_qT = [vw(arR, 0 + i * 2048, 2048, BF, pat="p (h t) -> p h t", h=4, parts=(0, 64)) for i in range(2)]
        a_kT = vw(arR, 4096, 2304, BF, pat="p (g t) -> p g t", g=4, parts=(0, 64))
        a_vA = vw(arR, 6400, 1152, BF, pat="p (t c) -> p t c", t=9)
        yaT = vw(arR, 7552, 4096, BF, pat="p (k t) -> p k t", k=8)
        a_t = [vw(arR, 11648 + i * 1024, 1024) for i in range(2)]
        a_p = [vw(arR, 13696 + i * 512, 512, BF) for i in range(2)]
        a_den = vw(arR, 14720, 256)
        a_rd = vw(arR, 14976, 256)
        assert 15232 <= 16000
        swa_after = prologue_res
        R_qT = [Res("qT%d" % i, after=swa_after) for i in range(2)]
        R_kT = Res("kT", after=swa_after)
        R_vA = Res("vA", after=swa_after)
        R_yaT = [Res("yaT%d" % k, after=swa_after) for k in range(8)]
        R_at = [Res("at%d" % i, after=swa_after) for i in range(2)]
        R_ap = [Res("ap%d" % i, after=swa_after) for i in range(2)]
        R_den = Res("den", after=swa_after)

        wkv, wkv_res = std_unit(w_in, IN_W, C_KA, 512, gmix)
        tokchunks = [(0, 128), (128, 640), (640, 1152)]
        for gp in range(2):
            for (a0, a1) in tokchunks:
                bk, bkr = next_bank()
                n = a1 - a0
                S.op("pe", lambda e, bk=bk, gp=gp, a0=a0, a1=a1, n=n: mm_group(
                    e, bk[:, 0:n], [(wkv[:, kc, gp * 128:(gp + 1) * 128], uT[:, kc, a0:a1]) for kc in range(16)]),
                    reads=uT_res(a0 // 128, (a1 + 127) // 128) + [wkv_res], writes=[bkr])
                S.op("dve", lambda e, bk=bk, gp=gp, a0=a0, a1=a1, n=n: e.tensor_copy(a_kT[:, 2 * gp, a0:a1], bk[0:64, 0:n]),
                     reads=[bkr], writes=[R_kT])
                S.op("act", lambda e, bk=bk, gp=gp, a0=a0, a1=a1, n=n: e.activation(out=a_kT[:, 2 * gp + 1, a0:a1], in_=bk[64:128, 0:n], func=AF.Copy),
                     reads=[bkr], writes=[R_kT])
        for t in range(9):
            bk, bkr = next_bank()
            S.op("pe", lambda e, bk=bk, t=t: mm_group(
                e, bk[:, 0:256], [(uT[:, kc, t * 128:(t + 1) * 128], wkv[:, kc, 256:512]) for kc in range(16)]),
                reads=[R_uT[t], wkv_res], writes=[bkr])
            S.op("dve", lambda e, bk=bk, t=t: e.tensor_copy(a_vA[:, t, :], bk[:, 0:256]), reads=[bkr], writes=[R_vA])

        swa_i = [0]
        for qu in range(2):
            wq, wq_res = std_unit(w_in, IN_W, C_QA + qu * 512, 512, gmix)
            for gl in range(2):
                for c2 in range(2):
                    cc = gl * 2 + c2
                    for hf in range(2):
                        bk, bkr = next_bank()
                        S.op("pe", lambda e, bk=bk, cc=cc, hf=hf: mm_group(
                            e, bk[:, :], [(wq[:, kc, cc * 128:(cc + 1) * 128], uT[:, kc, 128 + hf * 512:128 + (hf + 1) * 512]) for kc in range(16)]),
                            reads=uT_res(1 + hf * 4, 5 + hf * 4) + [wq_res], writes=[bkr])
                        S.op("dve", lambda e, bk=bk, gl=gl, c2=c2, hf=hf: e.tensor_scalar(
                            a_qT[gl][:, 2 * c2, hf * 512:(hf + 1) * 512], bk[0:64, :], 0.125, None, ALU.mult),
                            reads=[bkr], writes=[R_qT[gl]])
                        S.op("act", lambda e, bk=bk, gl=gl, c2=c2, hf=hf: e.activation(
                            out=a_qT[gl][:, 2 * c2 + 1, hf * 512:(hf + 1) * 512], in_=bk[64:128, :], func=AF.Copy, scale=0.125),
                            reads=[bkr], writes=[R_qT[gl]])
            for gl in range(2):
                g_ = qu * 2 + gl
                for blk in range(8):
                    i = swa_i[0] % 2
                    swa_i[0] += 1
                    tA, pA = a_t[i], a_p[i]
                    bkA, bkAr = next_bank()
                    bkB, bkBr = next_bank()
                    qs = a_qT[gl][:, :, blk * 128:(blk + 1) * 128]
                    S.op("pe", lambda e, bkA=bkA, g_=g_, blk=blk, qs=qs: e.matmul(
                        bkA[:, :], lhsT=a_kT[:, g_, blk * 128:(blk + 1) * 128], rhs=qs, start=True, stop=True),
                        reads=[R_kT, R_qT[gl]], writes=[bkAr])
                    S.op("pe", lambda e, bkB=bkB, g_=g_, blk=blk, qs=qs: e.matmul(
                        bkB[:, :], lhsT=a_kT[:, g_, (blk + 1) * 128:(blk + 2) * 128], rhs=qs, start=True, stop=True),
                        reads=[R_kT, R_qT[gl]], writes=[bkBr])
                    S.op("dve", lambda e, bkA=bkA, tA=tA, g_=g_: e.tensor_tensor(tA[:, 0:512], bkA[:, :], mb[:, g_, 0, :], ALU.add),
                         reads=[bkAr, R_mb], writes=[R_at[i]])
                    S.op("dve", lambda e, bkB=bkB, tA=tA, g_=g_: e.tensor_tensor(tA[:, 512:1024], bkB[:, :], mb[:, g_, 1, :], ALU.add),
                         reads=[bkBr, R_mb], writes=[R_at[i]])
                    S.op("act", lambda e, tA=tA, pA=pA: e.activation(out=pA, in_=tA, func=AF.Exp), reads=[R_at[i]], writes=[R_ap[i]])
                    if blk == 0:
                        S.op("dve", lambda e, pA=pA: e.tensor_scalar(pA[:, 0:512], pA[:, 0:512], flag[:, 0:1], None, ALU.mult),
                             reads=[R_ap[i], R_const], writes=[R_ap[i]])
                    bkO, bkOr = next_bank()

                    def pv(e, bkO=bkO, pA=pA, g_=g_, blk=blk):
                        inst = None
                        for hh in range(4):
                            po = (hh % 2) * 64
                            oo = bkO[po:po + 64, (hh // 2) * 128:(hh // 2 + 1) * 128]
                            e.matmul(oo, lhsT=a_vA[:, blk, g_ * 64:(g_ + 1) * 64], rhs=pA[:, hh * 128:(hh + 1) * 128], start=True, stop=False)
                            e.matmul(oo, lhsT=a_vA[:, blk + 1, g_ * 64:(g_ + 1) * 64], rhs=pA[:, 512 + hh * 128:512 + (hh + 1) * 128], start=False, stop=True)
                        for hh in range(4):
                            po = (hh % 2) * 64
                            so = bkO[po:po + 64, 256 + (hh // 2) * 128:256 + (hh // 2 + 1) * 128]
                            e.matmul(so, lhsT=ones_bf, rhs=pA[:, hh * 128:(hh + 1) * 128], start=True, stop=False)
                            inst = e.matmul(so, lhsT=ones_bf, rhs=pA[:, 512 + hh * 128:512 + (hh + 1) * 128], start=False, stop=True)
                        return inst
                    S.op("pe", pv, reads=[R_ap[i], R_vA, R_const], writes=[bkOr])
                    S.op("dve", lambda e, bkO=bkO, g_=g_: e.tensor_tensor(
                        a_den.rearrange("p (s q) -> p s q", s=2), bkO[:, 256:512].rearrange("p (s q) -> p s q", s=2),
                        esb[:, 2 * g_:2 * g_ + 2, :], ALU.add), reads=[bkOr, R_const], writes=[R_den])
                    S.op("act", lambda e: e.activation(out=a_rd, in_=a_den, func=AF.Ln), reads=[R_den], writes=[R_den])
                    S.op("act", lambda e: e.activation(out=a_rd, in_=a_rd, func=AF.Exp, scale=-1.0), reads=[R_den], writes=[R_den])
                    S.op("dve", lambda e, bkO=bkO, g_=g_, blk=blk: e.tensor_tensor(
                        yaT[:, 2 * g_:2 * g_ + 2, blk * 128:(blk + 1) * 128], bkO[:, 0:256].rearrange("p (s q) -> p s q", s=2),
                        a_rd.rearrange("p (s q) -> p s q", s=2), ALU.mult),
                        reads=[bkOr, R_den], writes=[R_yaT[2 * g_], R_yaT[2 * g_ + 1]])
        if "yaT" in dbg_t:
            tmpf = vw(xR, 0, 4096)
            for k in range(8):
                for hf in range(0, 1024, 512):
                    pass
            for k in range(8):
                rr = Res("dbgtmp%d" % k)
                t4 = vw(xR, 0, 1024)
                S.op("dve", lambda e, k=k, t4=t4: e.tensor_copy(t4, yaT[:, k, :]), reads=[R_yaT[k]] + xst_res + ubf_res, writes=xst_res + ubf_res)
                out_dbg("yaT", t4, xst_res, dram_ap=dbg_t["yaT"].ap()[k * 128:(k + 1) * 128, :])

        if stop == 'swa':
            S.emit(final_wait_ops=final_ops)
            build.stats = (len(S.ops), S.n_waits, len(S.sem_total))
            return nc
        swa_res = R_qT + [R_kT, R_vA] + R_at + R_ap + [R_den]
        obT = vw(arR, 11648, 4096, BF, pat="p (k t) -> p k t", k=8)
        R_obT = [Res("obT%d" % k, after=swa_res) for k in range(8)]
        g_qd = vw(arR, 0, 512, BF)
        g_kd = vw(arR, 512, 512, BF)
        g_kstT = vw(arR, 1024, 512, BF)
        g_kst = vw(arR, 1536, 512, BF, pat="p (t d) -> p t d", t=8)
        g_v = vw(arR, 2048, 1024, BF, pat="p (t c) -> p t c", t=8)
        g_G = vw(arR, 3072, 1024, BF, pat="p (t c) -> p t c", t=8)
        g_Sall = vw(arR, 4096, 2176, BF, pat="p (n v) -> p n v", n=17)
        g_glT = vw(arR, 6272, 512, BF, parts=(0, 16))
        g_ob = [vw(arR, 6784 + i * 128, 128, BF) for i in range(2)]
        g_att = [vw(arR, 7040 + i * 64, 64, BF) for i in range(2)]
        assert 7168 <= 7552
        g_eb = vw(xR, 0, 1024)
        g_einv = vw(xR, 1024, 1024)
        g_kd32 = vw(xR, 3072, 512)
        g_qd0 = vw(xR, 2048, 512, BF)
        g_qd1 = vw(xR, 2560, 512, BF)
        g_dec = vw(xR, 3584, 16)
        g_sg = [vw(xR, 3600 + i * 256, 256) for i in range(2)]
        g_junk = vw(xR, 4112, 256)
        x_all = xst_res + ubf_res
        R_gt = Res("gtmp", after=x_all)
        R_gkd32 = Res("gkd32", after=x_all)
        R_gqd01 = Res("gqd01", after=x_all)
        S.op("pool", lambda e: e.memset(g_qd0, 0.0), writes=[R_gqd01])
        S.op("pool", lambda e: e.memset(g_qd1, 0.0), reads=[R_gqd01], writes=[R_gqd01])
        R_gsg = [Res("gsg%d" % i, after=x_all) for i in range(2)]
        R_gjunk = Res("gjunk", after=x_all)
        R_gqd = Res("gqd", after=swa_res)
        R_gkd = Res("gkd", after=swa_res)
        R_gkstT = Res("gkstT", after=swa_res)
        R_gkst = Res("gkst", after=swa_res)
        R_gv = [Res("gv%d" % t, after=swa_res) for t in range(8)]
        R_gG = [Res("gG%d" % t, after=swa_res) for t in range(8)]
        R_gSall = Res("gSall", after=swa_res)
        R_gglT = Res("gglT", after=swa_res)
        R_gob = [Res("gob%d" % i, after=swa_res) for i in range(2)]
        R_gatt = [Res("gatt%d" % i, after=swa_res) for i in range(2)]

        wg_, wg_res = std_unit(w_in, IN_W, C_G, 16, gmix)
        for hf in range(2):
            bk, bkr = next_bank()
            S.op("pe", lambda e, bk=bk, hf=hf: mm_group(
                e, bk[0:16, :], [(wg_[:, kc, :], uT[:, kc, 128 + hf * 512:128 + (hf + 1) * 512]) for kc in range(16)]),
                reads=uT_res(1 + hf * 4, 5 + hf * 4) + [wg_res], writes=[bkr])
            S.op("act", lambda e, bk=bk, hf=hf: e.activation(out=g_glT[:, hf * 512:(hf + 1) * 512], in_=bk[0:16, :], func=AF.Copy),
                 reads=[bkr], writes=[R_gglT])
        gla_i = [0]
        for h in range(4):
            if stop == 'gla_g' and h == 0:
                S.emit(final_wait_ops=final_ops)
                build.stats = (len(S.ops), S.n_waits, len(S.sem_total))
                return nc
            gla_factors(h, g_glT, R_gglT, 1024, g_eb, g_einv, g_dec, R_gt, True)
            if stop == 'gla_f' and h == 0:
                S.emit(final_wait_ops=final_ops)
                build.stats = (len(S.ops), S.n_waits, len(S.sem_total))
                return nc
            pieces = []
            for (c0, off) in ((C_QB + h * 128, 0), (C_KB + h * 128, 128)):
                for p0 in range(0, 16, 4):
                    dap = bass.AP(w_in, (p0 * 128) * IN_W + c0, [[IN_W, 128], [128 * IN_W, 4], [1, 128]])
                    pieces.append((dap, (4, 128), (lambda base, p0=p0, off=off: wslot(base, 16, 256)[:, p0:p0 + 4, off:off + 128]), (gmix, p0)))
            base, wqk_res = load_unit(pieces, 2048)
            wqk = wslot(base, 16, 256)
            for hf in range(2):
                tsl = slice(hf * 512, (hf + 1) * 512)
                bk, bkr = next_bank()
                S.op("pe", lambda e, bk=bk, hf=hf: mm_group(
                    e, bk[:, :], [(wqk[:, kc, 0:128], uT[:, kc, 128 + hf * 512:128 + (hf + 1) * 512]) for kc in range(16)]),
                    reads=uT_res(1 + hf * 4, 5 + hf * 4) + [wqk_res], writes=[bkr])
                S.op("dve", lambda e, bk=bk, tsl=tsl: e.scalar_tensor_tensor(
                    g_qd[:, tsl], bk[:, :], float(128 ** -0.5), g_eb[:, tsl], ALU.mult, ALU.mult),
                    reads=[bkr, R_gt], writes=[R_gqd])
                bk, bkr = next_bank()
                S.op("pe", lambda e, bk=bk, hf=hf: mm_group(
                    e, bk[:, :], [(wqk[:, kc, 128:256], uT[:, kc, 128 + hf * 512:128 + (hf + 1) * 512]) for kc in range(16)]),
                    reads=uT_res(1 + hf * 4, 5 + hf * 4) + [wqk_res], writes=[bkr])
                S.op("dve", lambda e, bk=bk, tsl=tsl: e.tensor_tensor(g_kd32, bk[:, :], g_einv[:, tsl], ALU.mult),
                     reads=[bkr, R_gt], writes=[R_gkd32])
                S.op("act", lambda e, tsl=tsl: e.activation(out=g_kd[:, tsl], in_=g_kd32, func=AF.Copy), reads=[R_gkd32], writes=[R_gkd])
                S.op("dve", lambda e, tsl=tsl, hf=hf: e.tensor_tensor(
                    g_kstT[:, tsl].rearrange("p (n c) -> p n c", c=64), g_kd32.rearrange("p (n c) -> p n c", c=64),
                    g_dec[:, hf * 8:(hf + 1) * 8][:, :, None].broadcast_to([128, 8, 64]), ALU.mult),
                    reads=[R_gkd32, R_gt], writes=[R_gkstT])
            S.op("pool", lambda e: e.tensor_copy(g_qd0.rearrange("p (t c s) -> p t c s", c=2, s=64)[:, :, 0, :],
                                                  g_qd.rearrange("p (t c s) -> p t c s", c=2, s=64)[:, :, 0, :]),
                 reads=[R_gqd, R_gqd01], writes=[R_gqd01])
            S.op("pool", lambda e: e.tensor_copy(g_qd1.rearrange("p (t c s) -> p t c s", c=2, s=64)[:, :, 1, :],
                                                  g_qd.rearrange("p (t c s) -> p t c s", c=2, s=64)[:, :, 1, :]),
                 reads=[R_gqd, R_gqd01], writes=[R_gqd01])
            bk, bkr = next_bank()
            bkb = bk[:].bitcast(BF).rearrange("p (k c) -> p k c", k=8)

            def trk2(e, bkb=bkb):
                inst = None
                for t in range(8):
                    inst = e.transpose(bkb[:, t, :], g_kstT[:, t * 128:(t + 1) * 128], ident)
                return inst
            S.op("pe", trk2, reads=[R_gkstT, R_const], writes=[bkr])
            S.op("act", lambda e, bkb=bkb: e.activation(out=g_kst, in_=bkb, func=AF.Copy), reads=[bkr], writes=[R_gkst])
            if stop == 'gla_qk' and h == 0:
                S.emit(final_wait_ops=final_ops)
                build.stats = (len(S.ops), S.n_waits, len(S.sem_total))
                return nc
            pieces = []
            for (c0, off) in ((C_VB + h * 256, 0), (C_OG + h * 256, 256)):
                for p0 in range(0, 16, 4):
                    dap = bass.AP(w_in, (p0 * 128) * IN_W + c0, [[IN_W, 128], [128 * IN_W, 4], [1, 256]])
                    pieces.append((dap, (4, 256), (lambda base, p0=p0, off=off: wslot(base, 16, 512)[:, p0:p0 + 4, off:off + 256]), (gmix, p0)))
            base, wvo_res = load_unit(pieces, 4096)
            wvo = wslot(base, 16, 512)
            for t in range(8):
                bk, bkr = next_bank()
                S.op("pe", lambda e, bk=bk, t=t: mm_group(
                    e, bk[:, 0:256], [(uT[:, kc, (t + 1) * 128:(t + 2) * 128], wvo[:, kc, 0:256]) for kc in range(16)]),
                    reads=[R_uT[t + 1], wvo_res], writes=[bkr])
                S.op("dve", lambda e, bk=bk, t=t: e.tensor_copy(g_v[:, t, :], bk[:, 0:256]), reads=[bkr], writes=[R_gv[t]])
                bk2, bkr2 = next_bank()
                S.op("pe", lambda e, bk2=bk2, t=t: mm_group(
                    e, bk2[:, 0:256], [(uT[:, kc, (t + 1) * 128:(t + 2) * 128], wvo[:, kc, 256:512]) for kc in range(16)]),
                    reads=[R_uT[t + 1], wvo_res], writes=[bkr2])
                si = t % 2
                S.op("act", lambda e, bk2=bk2, si=si: e.activation(out=g_sg[si], in_=bk2[:, 0:256], func=AF.Silu), reads=[bkr2], writes=[R_gsg[si]])
                S.op("dve", lambda e, si=si, t=t: e.tensor_tensor(g_G[:, t, :], g_sg[si], gnb, ALU.mult),
                     reads=[R_gsg[si], R_const], writes=[R_gG[t]])
            if stop == 'gla_vo' and h == 0:
                S.emit(final_wait_ops=final_ops)
                build.stats = (len(S.ops), S.n_waits, len(S.sem_total))
                return nc
            S.op("act", lambda e, h=h: e.activation(out=g_Sall[:, 0, :], in_=Sst[:, h, :], func=AF.Copy), reads=[R_S[h]], writes=[R_gSall])
            for n in range(16):
                t, hf = n // 2, n % 2
                bk, bkr = next_bank()
                S.op("pe", lambda e, bk=bk, t=t, hf=hf: e.matmul(
                    bk[:, 0:256], lhsT=g_kst[hf * 64:(hf + 1) * 64, t, :], rhs=g_v[hf * 64:(hf + 1) * 64, t, :], start=True, stop=True),
                    reads=[R_gkst, R_gv[t]], writes=[bkr])
                S.op("dve", lambda e, bk=bk, n=n, h=h: e.scalar_tensor_tensor(
                    Sst[:, h, :], Sst[:, h, :], g_dec[:, n:n + 1], bk[:, 0:256], ALU.mult, ALU.add),
                    reads=[bkr, R_gt, R_S[h]], writes=[R_S[h]])
                S.op("act", lambda e, n=n, h=h: e.activation(out=g_Sall[:, n + 1, :], in_=Sst[:, h, :], func=AF.Copy),
                     reads=[R_S[h]], writes=[R_gSall])
            if stop == 'gla_scan' and h == 0:
                S.emit(final_wait_ops=final_ops)
                build.stats = (len(S.ops), S.n_waits, len(S.sem_total))
                return nc
            for j in range(8):
                i = gla_i[0] % 2
                gla_i[0] += 1
                tsl = slice(j * 128, (j + 1) * 128)
                bk, bkr = next_bank()
                S.op("pe", lambda e, bk=bk, tsl=tsl: e.matmul(bk[:, 0:128], lhsT=g_kd[:, tsl], rhs=g_qd[:, tsl], start=True, stop=True),
                     reads=[R_gkd, R_gqd], writes=[bkr])
                S.op("dve", lambda e, bk=bk, i=i: e.tensor_tensor(g_att[i], bk[:, 0:128], maskT, ALU.mult),
                     reads=[bkr, R_const], writes=[R_gatt[i]])
                bko, bkor = next_bank()

                def omm(e, bko=bko, i=i, j=j):
                    e.matmul(bko[:, 0:256], lhsT=g_att[i], rhs=g_v[:, j, :], start=True, stop=False)
                    e.matmul(bko[:, 0:256], lhsT=g_qd0[:, j * 128:(j + 1) * 128], rhs=g_Sall[:, 2 * j, :], start=False, stop=False)
                    return e.matmul(bko[:, 0:256], lhsT=g_qd1[:, j * 128:(j + 1) * 128], rhs=g_Sall[:, 2 * j + 1, :],
                                    start=False, stop=True)
                S.op("pe", omm, reads=[R_gatt[i], R_gv[j], R_gqd01, R_gSall], writes=[bkor])
                c0, c1 = small_col(), small_col()
                S.op("act", lambda e, bko=bko, c0=c0: e.activation(out=g_junk, in_=bko[:, 0:256], func=AF.Square, accum_out=small[:, c0:c0 + 1]),
                     reads=[bkor], writes=[R_gjunk, R_smc[c0]])
                rstd_from_ss(c0, 256.0, c1)
                S.op("dve", lambda e, bko=bko, c1=c1, i=i, j=j: e.scalar_tensor_tensor(
                    g_ob[i], bko[:, 0:256], small[:, c1:c1 + 1], g_G[:, j, :], ALU.mult, ALU.mult),
                    reads=[bkor, R_smc[c1], R_gG[j]], writes=[R_gob[i]])
                bkt, bktr = next_bank()
                bktb = bkt[:].bitcast(BF).rearrange("p (k c) -> p k c", k=8)

                def tro(e, bktb=bktb, i=i):
                    e.transpose(bktb[:, 0, :], g_ob[i][:, 0:128], ident)
                    return e.transpose(bktb[:, 1, :], g_ob[i][:, 128:256], ident)
                S.op("pe", tro, reads=[R_gob[i], R_const], writes=[bktr])
                S.op("act", lambda e, bktb=bktb, h=h, tsl=tsl: e.activation(out=obT[:, 2 * h:2 * h + 2, tsl], in_=bktb[:, 0:2, :], func=AF.Copy),
                     reads=[bktr], writes=[R_obT[2 * h], R_obT[2 * h + 1]])
        if "obT" in dbg_t:
            for k in range(8):
                t4 = vw(xR, 4400, 1024)
                rr = Res("dbgtmpo")
                S.op("dve", lambda e, k=k, t4=t4: e.tensor_copy(t4, obT[:, k, :]), reads=[R_obT[k], R_gt, R_gkd32] + R_gsg + [R_gjunk] + x_all,
                     writes=x_all + [R_gjunk])
                out_dbg("obT", t4, x_all, dram_ap=dbg_t["obT"].ap()[k * 128:(k + 1) * 128, :])

        if stop == 'gla':
            S.emit(final_wait_ops=final_ops)
            build.stats = (len(S.ops), S.n_waits, len(S.sem_total))
            return nc
        gla_res = [R_gqd, R_gkd, R_gkstT, R_gkst, R_gSall, R_gglT] + R_gv + R_gG + R_gob + R_gatt
        mT = vw(arR, 0, 7552, BF, pat="p (k t) -> p k t", k=16)[:, :, 0:1024] if False else None
        mT_a = vw(arR, 0, 7168, BF, pat="p (k t) -> p k t", k=14)
        mT_b = vw(xR, 0, 1024, BF, pat="p (k t) -> p k t", k=2)

        def mT_kc(kc):
            return mT_a[:, kc, :] if kc < 14 else mT_b[:, kc - 14, :]
        x_all2 = x_all + [R_gt, R_gkd32, R_gqd01] + R_gsg + [R_gjunk]
        R_mT = [Res("mT%d" % k, after=(gla_res if k < 14 else x_all2)) for k in range(16)]
        c_sa = [vw(xR, 1024 + i * 512, 512) for i in range(2)]
        c_ma = [vw(xR, 2048 + i * 512, 512) for i in range(2)]
        c_sb = [vw(xR, 3072 + i * 512, 512) for i in range(2)]
        c_mb = [vw(xR, 4096 + i * 512, 512) for i in range(2)]
        R_csa = [Res("csa%d" % i, after=x_all2) for i in range(2)]
        R_cma = [Res("cma%d" % i, after=x_all2) for i in range(2)]
        R_csb = [Res("csb%d" % i, after=x_all2) for i in range(2)]
        R_cmb = [Res("cmb%d" % i, after=x_all2) for i in range(2)]
        c_i = [0]
        for fc in range(16):
            def dstf(off, nk):
                return lambda base: vw(wbfR, base + off, nk * 64, BF, pat="p (k c) -> p k c", k=nk)
            pieces = [
                (bass.AP(w_in, C_GA + fc * 128, [[IN_W, 128], [128 * IN_W, 16], [1, 128]]), (16, 128), dstf(0, 16), (gmix, 0)),
                (bass.AP(w_in, C_GB + fc * 128, [[IN_W, 128], [128 * IN_W, 16], [1, 128]]), (16, 128), dstf(1024, 16), (gmix, 0)),
                (bass.AP(w_pa, fc * 128, [[D, 128], [128 * D, 8], [1, 128]]), (8, 128), dstf(2048, 8), None),
                (bass.AP(w_pb, fc * 128, [[D, 128], [128 * D, 8], [1, 128]]), (8, 128), dstf(2560, 8), None),
            ]
            base, wc_res = load_unit(pieces, 3072)
            wga = vw(wbfR, base, 1024, BF, pat="p (k c) -> p k c", k=16)
            wgb = vw(wbfR, base + 1024, 1024, BF, pat="p (k c) -> p k c", k=16)
            wpa_ = vw(wbfR, base + 2048, 512, BF, pat="p (k c) -> p k c", k=8)
            wpb_ = vw(wbfR, base + 2560, 512, BF, pat="p (k c) -> p k c", k=8)
            for hf in range(2):
                i = c_i[0] % 2
                c_i[0] += 1
                ts0, ts1 = hf * 512, (hf + 1) * 512
                bga, bgar = next_bank()
                S.op("pe", lambda e, bga=bga, wga=wga, ts0=ts0, ts1=ts1: mm_group(
                    e, bga[:, :], [(wga[:, kc, :], uT[:, kc, 128 + ts0:128 + ts1]) for kc in range(16)]),
                    reads=uT_res(1 + hf * 4, 5 + hf * 4) + [wc_res], writes=[bgar])
                bya, byar = next_bank()
                S.op("pe", lambda e, bya=bya, wpa_=wpa_, ts0=ts0, ts1=ts1: mm_group(
                    e, bya[:, :], [(wpa_[:, kc, :], yaT[:, kc, ts0:ts1]) for kc in range(8)]),
                    reads=R_yaT + [wc_res], writes=[byar])
                bgb, bgbr = next_bank()
                S.op("pe", lambda e, bgb=bgb, wgb=wgb, ts0=ts0, ts1=ts1: mm_group(
                    e, bgb[:, :], [(wgb[:, kc, :], uT[:, kc, 128 + ts0:128 + ts1]) for kc in range(16)]),
                    reads=uT_res(1 + hf * 4, 5 + hf * 4) + [wc_res], writes=[bgbr])
                byb, bybr = next_bank()
                S.op("pe", lambda e, byb=byb, wpb_=wpb_, ts0=ts0, ts1=ts1: mm_group(
                    e, byb[:, :], [(wpb_[:, kc, :], obT[:, kc, ts0:ts1]) for kc in range(8)]),
                    reads=R_obT + [wc_res], writes=[bybr])
                S.op("act", lambda e, bga=bga, i=i: e.activation(out=c_sa[i], in_=bga[:, :], func=AF.Sigmoid), reads=[bgar], writes=[R_csa[i]])
                S.op("dve", lambda e, bya=bya, i=i: e.tensor_tensor(c_ma[i], bya[:, :], c_sa[i], ALU.mult), reads=[byar, R_csa[i]], writes=[R_cma[i]])
                S.op("act", lambda e, bgb=bgb, i=i: e.activation(out=c_sb[i], in_=bgb[:, :], func=AF.Sigmoid), reads=[bgbr], writes=[R_csb[i]])
                S.op("dve", lambda e, byb=byb, i=i: e.tensor_tensor(c_mb[i], byb[:, :], c_sb[i], ALU.mult), reads=[bybr, R_csb[i]], writes=[R_cmb[i]])
                S.op("dve", lambda e, i=i, fc=fc, ts0=ts0, ts1=ts1: e.tensor_tensor(mT_kc(fc)[:, ts0:ts1], c_ma[i], c_mb[i], ALU.add),
                     reads=[R_cma[i], R_cmb[i]], writes=[R_mT[fc]])

        if stop == 'c':
            S.emit(final_wait_ops=final_ops)
            build.stats = (len(S.ops), S.n_waits, len(S.sem_total))
            return nc
        def h_tile(t):
            return vw(arR, 7552 + t * 2048, 2048) if t < 4 else vw(utR, (t - 4) * 2048, 2048)
        c_res = R_yaT + R_obT
        R_h = [Res("h%d" % t, after=(c_res if t < 4 else R_uT)) for t in range(8)]
        for t in range(8):
            S.dma(lambda e, t=t: e.dma_start(out=h_tile(t), in_=x_own.ap()[t * 128:(t + 1) * 128, :]), writes=[R_h[t]], key="hx%d" % t)
        for c in range(4):
            wo, wo_res = std_unit(w_out, D, c * 512, 512, None)
            for t in range(8):
                bk, bkr = next_bank()
                S.op("pe", lambda e, bk=bk, t=t, wo=wo: mm_group(
                    e, bk[:, :], [(mT_kc(kc)[:, t * 128:(t + 1) * 128], wo[:, kc, :]) for kc in range(16)]),
                    reads=R_mT + [wo_res], writes=[bkr])
                S.op("dve", lambda e, bk=bk, t=t, c=c: e.tensor_tensor(
                    h_tile(t)[:, c * 512:(c + 1) * 512], bk[:, :], h_tile(t)[:, c * 512:(c + 1) * 512], ALU.add),
                    reads=[bkr, R_h[t]], writes=[R_h[t]])
        if "h" in dbg_t:
            for t in range(8):
                out_dbg("h", h_tile(t), [R_h[t]], dram_ap=dbg_t["h"].ap()[t * 128:(t + 1) * 128, :])

        if stop == 'd':
            S.emit(final_wait_ops=final_ops)
            build.stats = (len(S.ops), S.n_waits, len(S.sem_total))
            return nc
        zT_a = vw(arR, 0, 7168, BF, pat="p (k t) -> p k t", k=14)
        zT_b = vw(xR, 0, 1024, BF, pat="p (k t) -> p k t", k=2)
        xr_users = x_all2 + R_csa + R_cma + R_csb + R_cmb + R_mT[14:16]
        e_after = R_mT + xr_users
        R_zT = [Res("zT%d" % t, after=e_after) for t in range(8)]
        R_ub2 = [Res("ub2_%d" % i, after=xr_users) for i in range(2)]

        def zT_kc(kc):
            return zT_a[:, kc, :] if kc < 14 else zT_b[:, kc - 14, :]

        for t in range(8):
            i = t % 2
            ub = vw(xR, 4096 + i * 1024, 1024, BF)
            c0, c1 = small_col(), small_col()
            ht = h_tile(t)
            S.op("act", lambda e, ub=ub, ht=ht, c0=c0: e.activation(out=ub, in_=ht, func=AF.Square, accum_out=small[:, c0:c0 + 1]),
                 reads=[R_h[t]], writes=[R_ub2[i], R_smc[c0]])
            rstd_from_ss(c0, 2048.0, c1)
            S.op("act", lambda e, ub=ub, ht=ht, c1=c1: e.activation(out=ub, in_=ht, func=AF.Copy, scale=small[:, c1:c1 + 1]),
                 reads=[R_h[t], R_smc[c1]], writes=[R_ub2[i]])
            tsl = slice(t * 128, (t + 1) * 128)
            for hb in range(2):
                bk, bkr = next_bank()
                bkb = bk[:].bitcast(BF).rearrange("p (k c) -> p k c", k=8)

                def tr(e, bkb=bkb, hb=hb, ub=ub):
                    inst = None
                    for k in range(8):
                        kc = hb * 8 + k
                        inst = e.transpose(bkb[:, k, :], ub[:, kc * 128:(kc + 1) * 128], ident)
                    return inst
                S.op("pe", tr, reads=[R_ub2[i], R_const], writes=[bkr])
                if hb == 0:
                    S.op("dve", lambda e, bkb=bkb, tsl=tsl: e.tensor_copy(zT_a[:, 0:8, tsl], bkb), reads=[bkr], writes=[R_zT[t]])
                else:
                    S.op("act", lambda e, bkb=bkb, tsl=tsl: e.activation(out=zT_a[:, 8:14, tsl], in_=bkb[:, 0:6, :], func=AF.Copy),
                         reads=[bkr], writes=[R_zT[t]])
                    S.op("dve", lambda e, bkb=bkb, tsl=tsl: e.tensor_copy(zT_b[:, :, tsl], bkb[:, 6:8, :]), reads=[bkr], writes=[R_zT[t]])

        if stop == 'e':
            S.emit(final_wait_ops=final_ops)
            build.stats = (len(S.ops), S.n_waits, len(S.sem_total))
            return nc
        hm = [vw(xR, 1024 + i * 2048, 2048, BF, pat="p (c t) -> p c t", c=4) for i in range(2)]
        f_sg = [vw(xR, 5120 + i * 512, 512) for i in range(2)]
        fg_after = R_ub2 + xr_users
        R_hm = [[Res("hm%d_%d" % (i, c), after=fg_after) for c in range(4)] for i in range(2)]
        R_fsg = [Res("fsg%d" % i, after=fg_after) for i in range(2)]
        NG = DFF // 512
        f_i = [0]
        for gI in range(NG):
            hi = gI % 2
            wg1, wg1_res = std_unit(w_fg, DFF, gI * 512, 512, gffn)
            wu1, wu1_res = std_unit(w_fu, DFF, gI * 512, 512, gffn)
            for cc in range(4):
                for hf in range(2):
                    i = f_i[0] % 2
                    f_i[0] += 1
                    ts0, ts1 = hf * 512, (hf + 1) * 512
                    bg_, bgr = next_bank()
                    S.op("pe", lambda e, bg_=bg_, wg1=wg1, cc=cc, ts0=ts0, ts1=ts1: mm_group(
                        e, bg_[:, :], [(wg1[:, kc, cc * 128:(cc + 1) * 128], zT_kc(kc)[:, ts0:ts1]) for kc in range(16)]),
                        reads=R_zT[hf * 4:(hf + 1) * 4] + [wg1_res], writes=[bgr])
                    bu_, bur = next_bank()
                    S.op("pe", lambda e, bu_=bu_, wu1=wu1, cc=cc, ts0=ts0, ts1=ts1: mm_group(
                        e, bu_[:, :], [(wu1[:, kc, cc * 128:(cc + 1) * 128], zT_kc(kc)[:, ts0:ts1]) for kc in range(16)]),
                        reads=R_zT[hf * 4:(hf + 1) * 4] + [wu1_res], writes=[bur])
                    S.op("act", lambda e, bg_=bg_, i=i: e.activation(out=f_sg[i], in_=bg_[:, :], func=AF.Silu), reads=[bgr], writes=[R_fsg[i]])
                    S.op("dve", lambda e, bu_=bu_, i=i, hi=hi, cc=cc, ts0=ts0, ts1=ts1: e.tensor_tensor(
                        hm[hi][:, cc, ts0:ts1], bu_[:, :], f_sg[i], ALU.mult), reads=[bur, R_fsg[i]], writes=[R_hm[hi][cc]])
            pieces = []
            for kc in range(4):
                dap = bass.AP(w_fd, (gI * 512 + kc * 128) * D, [[D, 128], [0, 1], [1, D]])
                pieces.append((dap, (1, D), (lambda base, kc=kc: wslot(base, 4, D)[:, kc:kc + 1, :]), None))
            base, wd_res = load_unit(pieces, 4096)
            wd = wslot(base, 4, D)
            last = (gI == NG - 1)
            if last:
                gfin = vw(wstR, 0, 2048)
                fjunk = vw(wstR, 2048, 1024, BF)
                R_gfin = Res("gfin", after=wst_res)
                R_fjunk = Res("fjunk", after=wst_res)
                S.dma(lambda e: e.dma_start(out=gfin, in_=bass.AP(gfin_d, 0, [[0, 128], [1, D]])), writes=[R_gfin], key="gfin")
            for t in range(8):
                for cb in range(4):
                    bk, bkr = next_bank()
                    S.op("pe", lambda e, bk=bk, t=t, cb=cb, wd=wd, hi=hi: mm_group(
                        e, bk[:, :], [(hm[hi][:, kc, t * 128:(t + 1) * 128], wd[:, kc, cb * 512:(cb + 1) * 512]) for kc in range(4)]),
                        reads=R_hm[hi] + [wd_res], writes=[bkr])
                    S.op("dve", lambda e, bk=bk, t=t, cb=cb: e.tensor_tensor(
                        h_tile(t)[:, cb * 512:(cb + 1) * 512], bk[:, :], h_tile(t)[:, cb * 512:(cb + 1) * 512], ALU.add),
                        reads=[bkr, R_h[t]], writes=[R_h[t]])
                if last:
                    c0, c1 = small_col(), small_col()
                    ht = h_tile(t)
                    S.op("act", lambda e, ht=ht, c0=c0: e.activation(out=fjunk, in_=ht, func=AF.Square, accum_out=small[:, c0:c0 + 1]),
                         reads=[R_h[t]], writes=[R_fjunk, R_smc[c0]])
                    rstd_from_ss(c0, 2048.0, c1)
                    S.op("dve", lambda e, ht=ht, c1=c1: e.scalar_tensor_tensor(ht, ht, small[:, c1:c1 + 1], gfin, ALU.mult, ALU.mult),
                         reads=[R_h[t], R_smc[c1], R_gfin], writes=[R_h[t]])
                    fo = S.dma(lambda e, ht=ht, t=t: e.dma_start(out=out_d.ap()[t * 128:(t + 1) * 128, :], in_=ht), reads=[R_h[t]],
                               writes=[Res("outd%d" % t)], key="out%d" % t)
                    final_ops.append(fo)
        S.emit(final_wait_ops=final_ops)
        build.stats = (len(S.ops), S.n_waits, len(S.sem_total))
    return nc


def _t5_bucket(d):
    d = np.maximum(d, 0)
    large = 16 + (np.log(np.maximum(d, 1).astype(np.float64) / 16.0) / np.log(128.0 / 16.0) * 16.0).astype(np.int64)
    large = np.minimum(large, 31)
    return np.where(d < 16, d, large)


def _mbias(rel_bias):
    rel_bias = np.asarray(rel_bias, np.float32)
    k = np.arange(128)[:, None]
    q = np.arange(128)[None, :]
    out = np.full((128, 4, 2, 4, 128), NEG, np.float32)
    for c in range(2):
        dist = q + 128 - (k + 128 * c)
        valid = (dist >= 0) & (dist < 128)
        bidx = _t5_bucket(dist)
        for h in range(16):
            vals = rel_bias[bidx, h]
            out[:, h // 4, c, h % 4, :] = np.where(valid, vals, np.float32(NEG))
    return np.ascontiguousarray(out.reshape(128, 4, 2, 512))


_CACHE = {}


def make_in_maps(x, norm_mix_g, w_in, sinks, rel_bias, w_gate_up, b_gate, gla_norm_g, w_proj_a, w_proj_b, w_out,
                 norm_ffn_g, w_ffn_gate, w_ffn_up, w_ffn_down, norm_final_g):
    f = lambda a: np.ascontiguousarray(np.asarray(a, dtype=np.float32))
    x = f(x)
    shared = {
        "w_in": f(w_in[0]), "w_proj_a": f(w_proj_a[0]), "w_proj_b": f(w_proj_b[0]), "w_out": f(w_out[0]),
        "w_ffn_gate": f(w_ffn_gate[0]), "w_ffn_up": f(w_ffn_up[0]), "w_ffn_down": f(w_ffn_down[0]),
        "gmix_cols": f(np.asarray(norm_mix_g[0]).reshape(16, 128).T),
        "gffn_cols": f(np.asarray(norm_ffn_g[0]).reshape(16, 128).T),
        "bgate_cols": f(np.asarray(b_gate[0]).reshape(4, 128).T),
        "gfinal": f(np.asarray(norm_final_g)[None, :]),
        "glanorm": f(np.asarray(gla_norm_g[0])[None, :]),
        "sinks_l": f(np.concatenate([np.broadcast_to(np.asarray(sinks[0])[0::2][None, :], (64, 8)),
                                     np.broadcast_to(np.asarray(sinks[0])[1::2][None, :], (64, 8))], axis=0)),
        "mbias": _mbias(rel_bias),
        "w_gate_up": f(w_gate_up[0]),
    }
    in_maps = []
    for c in range(8):
        b, j = c // 4, c % 4
        xp = np.zeros((NPREV, D), np.float32)
        if j > 0:
            xp[NPREV - j * TOK:] = x[b, 0:j * TOK]
        m = dict(shared)
        m["x_own"] = np.ascontiguousarray(x[b, j * TOK:(j + 1) * TOK])
        m["x_prev"] = xp
        m["flag"] = np.full((128, 1), 1.0 if j > 0 else 0.0, np.float32)
        in_maps.append(m)
    return in_maps


def kernel(**inputs):
    in_maps = make_in_maps(**inputs)
    if "nc" not in _CACHE:
        _CACHE["nc"] = build()
    res = run_bass_kernel_spmd(_CACHE["nc"], in_maps, core_ids=list(range(8)))
    out = np.stack([np.concatenate([res.results[b * 4 + j]["out"] for j in range(4)], axis=0) for b in range(2)], axis=0)
    return out.astype(np.float32)
```
